# Optimizing a Trainium2 kernel written in Bass

```python
import jax, jax.numpy as jnp
from jax import lax
import numpy as np

D_MODEL = 1024
BATCH = 1
SEQ = 16384
DEPTH = 4
DEC_BATCH = 16
DEC_SEQ = 64
PAST_LEN = 4096

CHUNK = 64
LEFT_CHUNKS = 8
WINDOW_A = LEFT_CHUNKS * CHUNK
H_A = 16
HD_A = D_MODEL // H_A
REL_CLIP = 128
N_REL = 2 * REL_CLIP + 1
H_B = 4
D_IN = D_MODEL
HD_B = D_IN // H_B
CONV_W = 4
D_FF = 4 * D_MODEL
N_A = (DEPTH + 1) // 2
N_B = DEPTH // 2
EPS = 1e-6
NEG = -1e30
F32 = jnp.float32

kernel_name = 'hybrid_chunkattn_mlstm_stream_step'


def rmsnorm(x, g):
    xf = x.astype(F32)
    y = xf * lax.rsqrt(jnp.mean(xf * xf, axis=-1, keepdims=True) + EPS)
    return (y * g.astype(F32)).astype(x.dtype)


def sq_relu_mlp(h, w_up, w_down):
    a = jax.nn.relu(h @ w_up)
    return (a * a) @ w_down


def attn_qkv(h, w_in, qg, kg):
    B, T, _ = h.shape
    q, k, v = jnp.split(h @ w_in, 3, axis=-1)
    q = rmsnorm(q.reshape(B, T, H_A, HD_A), qg)
    k = rmsnorm(k.reshape(B, T, H_A, HD_A), kg)
    return q, k, v.reshape(B, T, H_A, HD_A)


def band_attn(q, k, v, q_pos, k_pos, k_ok, rel_bias):
    s = jnp.einsum('bqhd,bkhd->bhqk', q, k).astype(F32) * (HD_A ** -0.5)
    rel = jnp.clip(q_pos[:, None] - k_pos[None, :], -REL_CLIP, REL_CLIP) + REL_CLIP
    s = s + rel_bias.astype(F32)[:, rel][None]
    s = jnp.where(k_ok[None, None, None, :], s, NEG)
    p = jax.nn.softmax(s, axis=-1).astype(v.dtype)
    return jnp.einsum('bhqk,bkhd->bqhd', p, v)


def attn_prompt(h, w_in, w_out, qg, kg, rel_bias):
    B, S, _ = h.shape
    q, k, v = attn_qkv(h, w_in, qg, kg)
    pad = ((0, 0), (WINDOW_A, 0), (0, 0), (0, 0))
    kp, vp = jnp.pad(k, pad), jnp.pad(v, pad)
    band = CHUNK + WINDOW_A

    def one_chunk(c):
        start = c * CHUNK
        qc = lax.dynamic_slice_in_dim(q, start, CHUNK, axis=1)
        kc = lax.dynamic_slice_in_dim(kp, start, band, axis=1)
        vc = lax.dynamic_slice_in_dim(vp, start, band, axis=1)
        q_pos = start + jnp.arange(CHUNK)
        k_pos = start - WINDOW_A + jnp.arange(band)
        return band_attn(qc, kc, vc, q_pos, k_pos, k_pos >= 0, rel_bias)

    o = lax.map(one_chunk, jnp.arange(S // CHUNK))
    o = jnp.moveaxis(o, 0, 1).reshape(B, S, D_MODEL)
    keep = min(WINDOW_A, S)
    return o @ w_out, k[:, S - keep:], v[:, S - keep:]


def attn_sample(h, ck, cv, w_in, w_out, qg, kg, rel_bias):
    B, T, _ = h.shape
    q, k, v = attn_qkv(h, w_in, qg, kg)
    Lc = ck.shape[1]
    kc = jnp.concatenate([ck.astype(k.dtype), k], axis=1)
    vc = jnp.concatenate([cv.astype(v.dtype), v], axis=1)
    q_pos = Lc + jnp.arange(T)
    k_pos = jnp.arange(Lc + T)
    o = band_attn(q, kc, vc, q_pos, k_pos, jnp.ones((Lc + T,), bool), rel_bias)
    return o.reshape(B, T, D_MODEL) @ w_out, k, v


def mlstm_chunk(carry, xs):
    C0, n0, m0 = carry
    q, k, v, ig, lf = xs
    L = q.shape[2]
    b = jnp.cumsum(lf, axis=-1)
    causal = jnp.tril(jnp.ones((L, L), bool))
    Dm = jnp.where(causal, b[..., :, None] - b[..., None, :] + ig[..., None, :], -jnp.inf)
    g = b + m0[..., None]
    m = jnp.maximum(g, jnp.max(Dm, axis=-1))
    S = jnp.einsum('bhtd,bhsd->bhts', q, k) * jnp.exp(Dm - m[..., None])
    inter = jnp.exp(g - m)
    num = jnp.einsum('bhts,bhse->bhte', S, v) + inter[..., None] * jnp.einsum('bhed,bhtd->bhte', C0, q)
    den = jnp.sum(S, axis=-1) + inter * jnp.einsum('bhd,bhtd->bht', n0, q)
    h = num / jnp.maximum(jnp.abs(den), jnp.exp(-m))[..., None]
    mL = m[..., -1]
    wS = jnp.exp(b[..., -1:] - b + ig - mL[..., None])
    decay = jnp.exp(b[..., -1] + m0 - mL)
    C = decay[..., None, None] * C0 + jnp.einsum('bhs,bhse,bhsd->bhed', wS, v, k)
    n = decay[..., None] * n0 + jnp.einsum('bhs,bhsd->bhd', wS, k)
    return (C, n, mL), h


def mlstm_mix(h, conv_buf, C0, n0, m0, w_in, b_i, b_f, cw, cb, hn, w_out):
    B, T, _ = h.shape
    z = h @ w_in
    qk_pre = z[..., :2 * D_IN]
    v = z[..., 2 * D_IN:3 * D_IN]
    o = z[..., 3 * D_IN:4 * D_IN]
    gi = z[..., 4 * D_IN:4 * D_IN + H_B]
    gf = z[..., 4 * D_IN + H_B:]
    xpad = jnp.concatenate([conv_buf.astype(qk_pre.dtype), qk_pre], axis=1)
    new_buf = xpad[:, T:]
    acc = cb
    for j in range(CONV_W):
        acc = acc + cw[j] * xpad[:, j:j + T]
    qk = jax.nn.silu(acc)

    def heads(a):
        return a.reshape(B, T, H_B, HD_B).transpose(0, 2, 1, 3).astype(F32)

    q = heads(qk[..., :D_IN])
    k = heads(qk[..., D_IN:]) * (HD_B ** -0.5)
    vv = heads(v)
    ig = (gi + b_i).astype(F32).transpose(0, 2, 1)
    lf = jax.nn.log_sigmoid((gf + b_f).astype(F32)).transpose(0, 2, 1)
    L = min(CHUNK, T)
    nc = T // L

    def blocks(a):
        return jnp.moveaxis(a.reshape((B, H_B, nc, L) + a.shape[3:]), 2, 0)

    (C, n, m), hs = lax.scan(mlstm_chunk, (C0.astype(F32), n0.astype(F32), m0.astype(F32)),
                             (blocks(q), blocks(k), blocks(vv), blocks(ig), blocks(lf)))
    hs = jnp.moveaxis(hs, 0, 2).reshape(B, H_B, T, HD_B).transpose(0, 2, 1, 3)
    hs = rmsnorm(hs, hn.reshape(H_B, HD_B)).astype(h.dtype)
    y = (hs * jax.nn.sigmoid(o).reshape(B, T, H_B, HD_B)).reshape(B, T, D_IN) @ w_out
    return y, C, n, m, new_buf


def setup_inputs(seed: int = 0) -> dict:
    key = jax.random.key(seed)
    ks = jax.random.split(key, 24)
    nrm = jax.random.normal
    a_len = min(WINDOW_A, PAST_LEN)
    d_in_b = 4 * D_IN + 2 * H_B
    return {
        'x_prompt': nrm(ks[0], (BATCH, SEQ, D_MODEL), F32),
        'x_sample': nrm(ks[1], (DEC_BATCH, DEC_SEQ, D_MODEL), F32),
        'cache_k': nrm(ks[2], (N_A, DEC_BATCH, a_len, H_A, HD_A), F32),
        'cache_v': nrm(ks[3], (N_A, DEC_BATCH, a_len, H_A, HD_A), F32),
        'state_C': nrm(ks[4], (N_B, DEC_BATCH, H_B, HD_B, HD_B), F32),
        'state_n': nrm(ks[5], (N_B, DEC_BATCH, H_B, HD_B), F32),
        'state_m': nrm(ks[6], (N_B, DEC_BATCH, H_B), F32),
        'state_conv': nrm(ks[7], (N_B, DEC_BATCH, CONV_W - 1, 2 * D_IN), F32),
        'norm_mix': 1.0 + 0.05 * nrm(ks[8], (DEPTH, D_MODEL), F32),
        'norm_ffn': 1.0 + 0.05 * nrm(ks[9], (DEPTH, D_MODEL), F32),
        'w_in_a': nrm(ks[10], (N_A, D_MODEL, 3 * D_MODEL), F32) * D_MODEL ** -0.5,
        'w_out_a': nrm(ks[11], (N_A, D_MODEL, D_MODEL), F32) * (0.5 * D_MODEL ** -0.5),
        'q_norm': 1.0 + 0.05 * nrm(ks[12], (N_A, HD_A), F32),
        'k_norm': 1.0 + 0.05 * nrm(ks[13], (N_A, HD_A), F32),
        'rel_bias': 0.2 * nrm(ks[14], (N_A, H_A, N_REL), F32),
        'w_in_b': nrm(ks[15], (N_B, D_MODEL, d_in_b), F32) * D_MODEL ** -0.5,
        'b_gate_i': 0.1 * nrm(ks[16], (N_B, H_B), F32),
        'b_gate_f': 3.0 + 0.5 * nrm(ks[17], (N_B, H_B), F32),
        'conv_w': 0.5 * nrm(ks[18], (N_B, CONV_W, 2 * D_IN), F32),
        'conv_b': 0.02 * nrm(ks[19], (N_B, 2 * D_IN), F32),
        'head_norm': 1.0 + 0.05 * nrm(ks[20], (N_B, D_IN), F32),
        'w_out_b': nrm(ks[21], (N_B, D_IN, D_MODEL), F32) * (0.5 * D_IN ** -0.5),
        'w_up': nrm(ks[22], (DEPTH, D_MODEL, D_FF), F32) * D_MODEL ** -0.5,
        'w_down': nrm(ks[23], (DEPTH, D_FF, D_MODEL), F32) * (0.5 * D_FF ** -0.5),
    }


def reference(x_prompt, x_sample, cache_k, cache_v, state_C, state_n, state_m, state_conv,
              norm_mix, norm_ffn, w_in_a, w_out_a, q_norm, k_norm, rel_bias,
              w_in_b, b_gate_i, b_gate_f, conv_w, conv_b, head_norm, w_out_b, w_up, w_down):
    xp, xs = x_prompt, x_sample
    Bp = xp.shape[0]
    kp_l, vp_l, ks_l, vs_l = [], [], [], []
    Cp_l, np_l, mp_l, bp_l = [], [], [], []
    Cs_l, ns_l, ms_l, bs_l = [], [], [], []
    for i in range(DEPTH):
        hp = rmsnorm(xp, norm_mix[i])
        hs = rmsnorm(xs, norm_mix[i])
        j = i // 2
        if i % 2 == 0:
            dp, k_p, v_p = attn_prompt(hp, w_in_a[j], w_out_a[j], q_norm[j], k_norm[j], rel_bias[j])
            ds, k_s, v_s = attn_sample(hs, cache_k[j], cache_v[j], w_in_a[j], w_out_a[j],
                                       q_norm[j], k_norm[j], rel_bias[j])
            kp_l.append(k_p); vp_l.append(v_p); ks_l.append(k_s); vs_l.append(v_s)
        else:
            prm = (w_in_b[j], b_gate_i[j], b_gate_f[j], conv_w[j], conv_b[j], head_norm[j], w_out_b[j])
            dp, C_p, n_p, m_p, b_p = mlstm_mix(
                hp, jnp.zeros((Bp, CONV_W - 1, 2 * D_IN), hp.dtype),
                jnp.zeros((Bp, H_B, HD_B, HD_B), F32), jnp.zeros((Bp, H_B, HD_B), F32),
                jnp.zeros((Bp, H_B), F32), *prm)
            ds, C_s, n_s, m_s, b_s = mlstm_mix(hs, state_conv[j], state_C[j], state_n[j], state_m[j], *prm)
            Cp_l.append(C_p); np_l.append(n_p); mp_l.append(m_p); bp_l.append(b_p)
            Cs_l.append(C_s); ns_l.append(n_s); ms_l.append(m_s); bs_l.append(b_s)
        xp = xp + dp
        xs = xs + ds
        xp = xp + sq_relu_mlp(rmsnorm(xp, norm_ffn[i]), w_up[i], w_down[i])
        xs = xs + sq_relu_mlp(rmsnorm(xs, norm_ffn[i]), w_up[i], w_down[i])
    return (xp, xs,
            jnp.stack(kp_l), jnp.stack(vp_l), jnp.stack(ks_l), jnp.stack(vs_l),
            jnp.stack(Cp_l), jnp.stack(np_l), jnp.stack(mp_l), jnp.stack(bp_l),
            jnp.stack(Cs_l), jnp.stack(ns_l), jnp.stack(ms_l), jnp.stack(bs_l))
```

```python
import numpy as np
from contextlib import ExitStack
import concourse.bass as bass
import concourse.mybir as mybir
from concourse.bass_utils import run_bass_kernel_spmd

F32 = mybir.dt.float32
BF16 = mybir.dt.bfloat16
AF = mybir.ActivationFunctionType
ALU = mybir.AluOpType
AX = mybir.AxisListType

NCORE = 8
D = 1024
KC = 8
HA = 16
HDA = 64
HB = 4
HDB = 256
CH = 64
WIN = 512
DFF = 4096
EPS = 1e-6
NEGM = -30000.0
NEGBIG = -1.0e30
A_INIT = -10000.0


def tiles_of(T, step=512):
    out = []
    t = 0
    while t < T:
        n = min(step, T - t)
        out.append((t, n))
        t += n
    return out


class Ctx:
    ENG = ('pe', 'act', 'dve', 'pool', 'sp')

    def __init__(self, nc, es):
        self.nc = nc
        self.es = es
        self.e = {'pe': nc.tensor, 'act': nc.scalar, 'dve': nc.vector, 'pool': nc.gpsimd, 'sp': nc.sync}
        self.sem = {k: es.enter_context(nc.semaphore('s_' + k)) for k in self.ENG}
        self.cnt = {k: 0 for k in self.ENG}
        self.pend = {k: False for k in self.ENG}
        self.known = {k: {} for k in self.ENG}
        self.lw = {}
        self.rd = {}
        self.ndma = 12
        self.dsem = {q: [es.enter_context(nc.semaphore('d_%s%d' % (q, i))) for i in range(self.ndma)]
                     for q in ('sp', 'pool')}
        self.dcnt = {q: [0] * self.ndma for q in ('sp', 'pool')}
        self.dnext = {'sp': 0, 'pool': 0}
        self.uid = 0
        self.rr = 0

    def tile(self, name, shape, dt, es=None):
        self.uid += 1
        return (es or self.es).enter_context(self.nc.sbuf_tensor('%s_%d' % (name, self.uid), list(shape), dt))

    def psum(self, name, shape, dt, es=None):
        self.uid += 1
        return (es or self.es).enter_context(self.nc.psum_tensor('%s_%d' % (name, self.uid), list(shape), dt))

    def _deps(self, eng, R, W):
        deps = {}

        def add(tok):
            k, v = tok
            if deps.get(k, 0) < v:
                deps[k] = v
        for r in R:
            if r in self.lw:
                add(self.lw[r])
        for w in W:
            if w in self.lw:
                d = self.lw[w]
                if d[0] != eng:
                    add(d)
            for k, v in self.rd.get(w, {}).items():
                if k != eng:
                    add((k, v))
        return deps

    def _semof(self, k):
        return self.sem[k] if isinstance(k, str) else self.dsem[k[1]][k[2]]

    def _wait(self, eng, deps):
        for k in sorted(deps, key=str):
            v = deps[k]
            if self.known[eng].get(k, 0) >= v:
                continue
            if isinstance(k, str):
                assert self.cnt[k] >= v, ("dep on unsignalled op", eng, k, v, self.cnt[k])
            self.e[eng].wait_ge(self._semof(k), v)
            self.known[eng][k] = v

    def _record(self, tok, R, W):
        k, v = tok
        for r in R:
            d = self.rd.setdefault(r, {})
            if d.get(k, 0) < v:
                d[k] = v
        for w in W:
            self.lw[w] = tok
            self.rd[w] = {}

    def op(self, eng, fn, R=(), W=(), sig=True):
        self._wait(eng, self._deps(eng, R, W))
        ins = fn()
        tok = (eng, self.cnt[eng] + 1)
        if sig:
            ins.then_inc(self.sem[eng], 1)
            self.cnt[eng] += 1
            self.pend[eng] = False
        else:
            self.pend[eng] = True
        self._record(tok, R, W)
        return ins

    def dma(self, q, out, in_, R=(), W=()):
        deps = self._deps(('q', q), R, W)
        s = self.dnext[q]
        self.dnext[q] = (s + 1) % self.ndma
        key = ('d', q, s)
        if self.dcnt[q][s] > 0:
            if deps.get(key, 0) < self.dcnt[q][s]:
                deps[key] = self.dcnt[q][s]
        self._wait(q, deps)
        ins = self.e[q].dma_start(out=out, in_=in_)
        self.dcnt[q][s] += 16
        ins.then_inc(self.dsem[q][s], 16)
        self._record((key, self.dcnt[q][s]), R, W)
        return ins

    def barrier(self):
        for k in self.ENG:
            assert not self.pend[k], k
        for eng in self.ENG:
            deps = {}
            for k in self.ENG:
                if k != eng and self.cnt[k] > 0:
                    deps[k] = self.cnt[k]
            for q in ('sp', 'pool'):
                for s in range(self.ndma):
                    if self.dcnt[q][s] > 0:
                        deps[('d', q, s)] = self.dcnt[q][s]
            self._wait(eng, deps)
        self.lw = {}
        self.rd = {}

    def finish(self):
        deps = {}
        for q in ('sp', 'pool'):
            for s in range(self.ndma):
                if self.dcnt[q][s] > 0:
                    deps[('d', q, s)] = self.dcnt[q][s]
        self._wait('sp', deps)

    def mm(self, out, lhsT, rhs, start, stop, R, W, sig=None):
        if sig is None:
            sig = stop
        return self.op('pe', lambda: self.nc.tensor.matmul(out, lhsT, rhs, start=start, stop=stop), R, W, sig)

    def tr(self, out, in_, ident, R, W, sig=True):
        return self.op('pe', lambda: self.nc.tensor.transpose(out, in_, ident), R, W, sig)

    def act(self, out, in_, func, R, W, bias=None, scale=None, accum_out=None):
        kw = {}
        if bias is not None:
            kw['bias'] = bias
        if scale is not None:
            kw['scale'] = scale
        if accum_out is not None:
            kw['accum_out'] = accum_out
        return self.op('act', lambda: self.nc.scalar.activation(out=out, in_=in_, func=func, **kw), R, W)

    def veng(self, sbuf_only=False):
        if sbuf_only:
            self.rr ^= 1
            return 'pool' if self.rr else 'dve'
        return 'dve'

    def tt(self, eng, out, in0, in1, op, R, W):
        return self.op(eng, lambda: self.e[eng].tensor_tensor(out=out, in0=in0, in1=in1, op=op), R, W)

    def ts(self, eng, out, in0, s1, s2, op0, op1, R, W, accum_out=None):
        if op1 is None:
            return self.op(eng, lambda: self.e[eng].tensor_scalar(out=out, in0=in0, scalar1=s1, scalar2=None, op0=op0), R, W)
        return self.op(eng, lambda: self.e[eng].tensor_scalar(out=out, in0=in0, scalar1=s1, scalar2=s2, op0=op0, op1=op1), R, W)

    def stt(self, eng, out, in0, scalar, in1, op0, op1, R, W):
        return self.op(eng, lambda: self.e[eng].scalar_tensor_tensor(out=out, in0=in0, scalar=scalar, in1=in1, op0=op0, op1=op1), R, W)

    def copy(self, eng, out, in_, R, W):
        if eng == 'act':
            return self.op('act', lambda: self.nc.scalar.copy(out=out, in_=in_), R, W)
        return self.op(eng, lambda: self.e[eng].tensor_copy(out=out, in_=in_), R, W)

    def memset(self, eng, ap, val, W):
        return self.op(eng, lambda: self.e[eng].memset(ap, val), (), W)


class Ring:
    def __init__(self, ctx, name, shape, dt, n, es=None, psum=False):
        self.bufs = [(ctx.psum if psum else ctx.tile)(name, shape, dt, es) for _ in range(n)]
        self.keys = ['%s#%d_%d' % (name, ctx.uid, i) for i in range(n)]
        self.i = 0

    def next(self):
        b, k = self.bufs[self.i], self.keys[self.i]
        self.i = (self.i + 1) % len(self.bufs)
        return b, k


def host_consts():
    c = np.zeros((128, 7 * 128), np.float32)
    c[:, 0:128] = np.eye(128, dtype=np.float32)
    c[0:64, 128:192] = 1.0
    c[64:128, 192:256] = 1.0
    c[:, 256:384] = 1.0
    s = np.arange(64)
    c[0:64, 384:448] = (s[:, None] <= s[None, :]).astype(np.float32)
    c[0:64, 512:576] = np.where(s[None, :] <= s[:, None], 0.0, NEGBIG)
    c[0:64, 640:704] = np.where(s[:, None] <= s[None, :], 0.0, NEGBIG)
    c[63, 768:896] = 1.0
    return c


class Consts:
    def __init__(self, ctx, cst_d):
        self.f = ctx.tile('cstf', [128, 7 * 128], F32)
        self.b = ctx.tile('cstb', [128, 7 * 128], BF16)
        ctx.dma('sp', self.f[:], cst_d[:, :], W=['cstf'])
        ctx.dma('pool', self.b[:], cst_d[:, :], W=['cstb'])
        self.col = ctx.tile('cstcol', [128, 4], F32)
        ctx.memset('dve', self.col[:, 0:1], EPS, W=['cstcol'])
        ctx.memset('dve', self.col[:, 1:2], 1.0, W=['cstcol'])
        ctx.memset('dve', self.col[:, 2:3], 0.0, W=['cstcol'])
        ctx.memset('dve', self.col[:, 3:4], 64.0 * EPS, W=['cstcol'])
        self.eps_col = self.col[:, 0:1]
        self.one_col = self.col[:, 1:2]
        self.zero_col = self.col[:, 2:3]
        self.eps64_col = self.col[:, 3:4]
        self.ident_b = self.b[:, 0:128]
        self.ident_f = self.f[:, 0:128]
        self.bd64_b = self.b[:, 128:256]
        self.ones_b = self.b[:, 256:384]
        self.ones_f = self.f[:, 256:384]
        self.U_f = self.f[0:64, 384:448]
        self.mneg_f = self.f[0:64, 512:576]
        self.mnegT_f = self.f[0:64, 640:704]
        self.sel63_f = self.f[0:64, 768:896]


def load_vec_cols(ctx, name, d_ap, ncol):
    t = ctx.tile(name, [128, ncol], F32)
    ctx.dma('sp', t[:], d_ap, W=[name])
    return t


def load_w(ctx, w_sb, key, w_d, c0, c1, q='pool'):
    src = w_d.rearrange("(k p) n -> p k n", p=128)[:, :, c0:c1]
    ctx.dma(q, w_sb, src, W=[key])


def rmsnorm_tile(ctx, cs, xap, xkey, g_sb, gkey, n, hap, hkey, sq_ring, ss_ring, rs_ring, nchunk=KC, dim=D):
    sq, sqk = sq_ring.next()
    ctx.act(sq[:, 0:nchunk, 0:n], xap, AF.Square, R=[xkey], W=[sqk])
    ss, ssk = ss_ring.next()
    for k in range(nchunk):
        ctx.mm(ss[:, 0:n], cs.ones_b, sq[:, k, 0:n], k == 0, k == nchunk - 1, R=[sqk, 'cstb'], W=[ssk])
    rs, rsk = rs_ring.next()
    ctx.act(rs[:, 0:n], ss[:, 0:n], AF.Ln, R=[ssk, 'cstcol'], W=[rsk], bias=cs.eps_col[:, 0:1], scale=1.0 / dim)
    ctx.act(rs[:, 0:n], rs[:, 0:n], AF.Exp, R=[rsk], W=[rsk], scale=-0.5)
    for k in range(nchunk):
        ctx.stt('dve', hap[:, k, :], xap[:, k, :], g_sb[:, k:k + 1], rs[:, 0:n], ALU.mult, ALU.mult,
                R=[xkey, gkey, rsk], W=[hkey])


def mlp_stage(ctx, cs, xT, NT, nf_d, wup_d, wdn_d, abufs=2):
    tl = tiles_of(NT)
    with ExitStack() as es:
        g = ctx.tile('gffn', [128, KC], F32, es)
        ctx.dma('sp', g[:], nf_d, W=['gffn'])
        hT = ctx.tile('hT', [128, KC, NT], BF16, es)
        sq_ring = Ring(ctx, 'sq', [128, KC, 512], BF16, 2, es)
        ss_ring = Ring(ctx, 'ssp', [128, 512], F32, 2, es, psum=True)
        rs_ring = Ring(ctx, 'rs', [128, 512], F32, 2, es)
        for i, (t0, n) in enumerate(tl):
            rmsnorm_tile(ctx, cs, xT[:, :, t0:t0 + n], 'x%d' % i, g, 'gffn', n, hT[:, :, t0:t0 + n], 'hT%d' % i,
                         sq_ring, ss_ring, rs_ring)
        GW = 512
        NG = DFF // GW
        wu_ring = Ring(ctx, 'wu', [128, KC, GW], BF16, 2, es)
        wd_ring = Ring(ctx, 'wd', [128, GW // 128, D], BF16, 2, es)
        a_ring = Ring(ctx, 'ag', [128, GW // 128, NT], BF16, abufs, es)
        r_ring = Ring(ctx, 'rl', [128, 512], F32, 3, es)
        pu_ring = Ring(ctx, 'pu', [128, 512], F32, 3, es, psum=True)
        pd_ring = Ring(ctx, 'pd', [128, 512], F32, 2, es, psum=True)
        for gi in range(NG):
            wu, wuk = wu_ring.next()
            load_w(ctx, wu[:], wuk, wup_d, gi * GW, (gi + 1) * GW)
            wd, wdk = wd_ring.next()
            ctx.dma('pool', wd[:], wdn_d[gi * GW:(gi + 1) * GW, :].rearrange("(c p) d -> p c d", p=128), W=[wdk])
            ag, agk = a_ring.next()
            for i, (t0, n) in enumerate(tl):
                for c in range(GW // 128):
                    pu, puk = pu_ring.next()
                    for k in range(KC):
                        ctx.mm(pu[:, 0:n], wu[:, k, c * 128:(c + 1) * 128], hT[:, k, t0:t0 + n], k == 0, k == KC - 1,
                               R=[wuk, 'hT%d' % i], W=[puk])
                    rl, rlk = r_ring.next()
                    ctx.act(rl[:, 0:n], pu[:, 0:n], AF.Relu, R=[puk], W=[rlk])
                    ctx.tt('pool', ag[:, c, t0:t0 + n], rl[:, 0:n], rl[:, 0:n], ALU.mult, R=[rlk], W=[agk + '_%d' % i])
            for i, (t0, n) in enumerate(tl):
                for dc in range(KC):
                    pd, pdk = pd_ring.next()
                    for c in range(GW // 128):
                        ctx.mm(pd[:, 0:n], wd[:, c, dc * 128:(dc + 1) * 128], ag[:, c, t0:t0 + n], c == 0,
                               c == GW // 128 - 1, R=[wdk, agk + '_%d' % i], W=[pdk])
                    ctx.tt('dve', xT[:, dc, t0:t0 + n], xT[:, dc, t0:t0 + n], pd[:, 0:n], ALU.add,
                           R=['x%d' % i, pdk], W=['x%d' % i])
    ctx.barrier()


def load_x(ctx, xT, xT_d, NT):
    for i, (t0, n) in enumerate(tiles_of(NT)):
        ctx.dma('sp', xT[:, :, t0:t0 + n], xT_d[:, :, t0:t0 + n].rearrange("k p t -> p k t"), W=['x%d' % i])


def store_x(ctx, xT, xo_d, NT):
    for i, (t0, n) in enumerate(tiles_of(NT)):
        ctx.dma('sp', xo_d[:, :, t0:t0 + n].rearrange("k p t -> p k t"), xT[:, :, t0:t0 + n], R=['x%d' % i])


def run_prog(nc, in_maps):
    res = run_bass_kernel_spmd(nc, in_maps, core_ids=list(range(NCORE)))
    return res.results


def new_nc():
    return bass.Bass("TRN2", target_bir_lowering=False)


def din(nc, name, shape, dt=F32):
    return nc.dram_tensor(name, list(shape), dt, kind="ExternalInput").ap()


def dout(nc, name, shape, dt=F32):
    return nc.dram_tensor(name, list(shape), dt, kind="ExternalOutput").ap()


def qkv_stage(ctx, cs, x_src, T, g_d, win_d, gq_d, gk_d, qT_o, kT_o, v_o, x_resident=None):
    with ExitStack() as es:
        g = ctx.tile('gmix', [128, KC], F32, es)
        ctx.dma('sp', g[:], g_d, W=['gmix'])
        gqk = ctx.tile('gqk', [128, 2], F32, es)
        ctx.dma('sp', gqk[:, 0:1], gq_d, W=['gqk'])
        ctx.dma('sp', gqk[:, 1:2], gk_d, W=['gqk'])
        ctx.ts('dve', gqk[:, 1:2], gqk[:, 1:2], 8.0, None, ALU.mult, None, R=['gqk'], W=['gqk'])
        w = ctx.tile('win', [128, KC, 3 * D], BF16, es)
        for c in range(6):
            load_w(ctx, w[:, :, c * 512:(c + 1) * 512], 'win%d' % c, win_d, c * 512, (c + 1) * 512)
        x_ring = Ring(ctx, 'xq', [128, KC, 512], F32, 2, es)
        h_ring = Ring(ctx, 'hq', [128, KC, 512], BF16, 2, es)
        sq_ring = Ring(ctx, 'sq', [128, KC, 512], BF16, 2, es)
        ss_ring = Ring(ctx, 'ssp', [128, 512], F32, 2, es, psum=True)
        rs_ring = Ring(ctx, 'rs', [128, 512], F32, 2, es)
        pq_ring = Ring(ctx, 'pq', [128, 512], F32, 3, es, psum=True)
        p2_ring = Ring(ctx, 'p2', [128, 512], F32, 2, es, psum=True)
        s2_ring = Ring(ctx, 's2', [128, 512], BF16, 2, es)
        t2_ring = Ring(ctx, 't2', [128, 512], F32, 2, es)
        st_ring = Ring(ctx, 'stq', [128, 512], F32, 3, es)
        for i, (t0, n) in enumerate(tiles_of(T)):
            if x_resident is not None:
                xap, xk = x_resident[:, :, t0:t0 + n], 'x%d' % i
            else:
                xt, xk = x_ring.next()
                ctx.dma('sp', xt[:, :, 0:n], x_src[:, :, t0:t0 + n].rearrange("k p t -> p k t"), W=[xk])
                xap = xt[:, :, 0:n]
            ht, hk = h_ring.next()
            rmsnorm_tile(ctx, cs, xap, xk, g, 'gmix', n, ht[:, :, 0:n], hk, sq_ring, ss_ring, rs_ring)
            for j in range(16):
                pq, pqk = pq_ring.next()
                for k in range(KC):
                    ctx.mm(pq[:, 0:n], w[:, k, j * 128:(j + 1) * 128], ht[:, k, 0:n], k == 0, k == KC - 1,
                           R=['win%d' % (j // 4), hk], W=[pqk])
                s2, s2k = s2_ring.next()
                ctx.act(s2[:, 0:n], pq[:, 0:n], AF.Square, R=[pqk], W=[s2k])
                p2, p2k = p2_ring.next()
                ctx.mm(p2[:, 0:n], cs.bd64_b, s2[:, 0:n], True, True, R=[s2k, 'cstb'], W=[p2k])
                t2, t2k = t2_ring.next()
                ctx.act(t2[:, 0:n], p2[:, 0:n], AF.Ln, R=[p2k, 'cstcol'], W=[t2k], bias=cs.eps64_col, scale=1.0)
                ctx.act(t2[:, 0:n], t2[:, 0:n], AF.Exp, R=[t2k], W=[t2k], scale=-0.5)
                st, stk = st_ring.next()
                isk = 1 if j >= 8 else 0
                ctx.stt('dve', st[:, 0:n], pq[:, 0:n], gqk[:, isk:isk + 1], t2[:, 0:n], ALU.mult, ALU.mult,
                        R=[pqk, 'gqk', t2k], W=[stk])
                dst = (kT_o if isk else qT_o)[j % 8, :, t0:t0 + n]
                ctx.dma('sp', dst, st[:, 0:n], R=[stk])
            for s0 in range(0, n, 128):
                for gv in range(2):
                    pq, pqk = pq_ring.next()
                    for k in range(KC):
                        ctx.mm(pq[:, :], ht[:, k, s0:s0 + 128], w[:, k, 2 * D + gv * 512:2 * D + (gv + 1) * 512],
                               k == 0, k == KC - 1, R=['win%d' % (4 + gv), hk], W=[pqk])
                    st, stk = st_ring.next()
                    ctx.copy('act', st[:, :], pq[:, :], R=[pqk], W=[stk])
                    ctx.dma('sp', v_o[t0 + s0:t0 + s0 + 128, gv * 512:(gv + 1) * 512], st[:, :], R=[stk])
    ctx.barrier()


def build_qkv(T):
    nc = new_nc()
    x_d = din(nc, "xT", [KC, 128, T])
    cst_d = din(nc, "cst", [128, 896])
    g_d = din(nc, "g", [128, KC])
    win_d = din(nc, "win", [D, 3 * D])
    gq_d = din(nc, "gq", [128, 1])
    gk_d = din(nc, "gk", [128, 1])
    qT_o = dout(nc, "qT", [8, 128, T])
    kT_o = dout(nc, "kT", [8, 128, T])
    v_o = dout(nc, "v", [T, D])
    with ExitStack() as es:
        ctx = Ctx(nc, es)
        cs = Consts(ctx, cst_d)
        qkv_stage(ctx, cs, x_d, T, g_d, win_d, gq_d, gk_d, qT_o, kT_o, v_o)
        ctx.finish()
    return nc


def attn_stage(ctx, cs, xT, P, qT_d, kT_d, v_d, ones_d, bt_d, wout_d):
    NT = P + 128
    NK = 512 + P + 1280
    NKT = NK // 128
    qtiles = [(qt * 128, 128, qt, False) for qt in range(P // 128)]
    for s in range(2):
        qtiles.append((P + 64 * s, 64, (512 + P) // 128 + 5 * s, True))
    with ExitStack() as es0:
        oT = ctx.tile('oT', [128, 8, NT], BF16, es0)
        with ExitStack() as es:
            ones_sb = ctx.tile('onesb', [128, NKT], BF16, es)
            ctx.dma('pool', ones_sb[:], ones_d, W=['onesb'])
            q_ring = Ring(ctx, 'qTp', [128, NT], BF16, 2, es)
            k_ring = Ring(ctx, 'kTp', [128, NK], BF16, 2, es)
            v_ring = Ring(ctx, 'vaug', [128, NKT, 2, 65], BF16, 2, es)
            b_ring = Ring(ctx, 'bt', [128, 2, 640], F32, 2, es)
            s_ring = Ring(ctx, 'S', [128, 1024], F32, 2, es, psum=True)
            n_ring = Ring(ctx, 'num', [128, 512], F32, 2, es, psum=True)
            tp_ring = Ring(ctx, 'tp', [128, 1024], BF16, 1, es, psum=True)
            tmp_ring = Ring(ctx, 'tmp', [128, 640], F32, 2, es)
            e_ring = Ring(ctx, 'e', [128, 640], BF16, 2, es)
            rd_ring = Ring(ctx, 'rd', [128, 1], F32, 2, es)
            ot_ring = Ring(ctx, 'otok', [128, 128], BF16, 2, es)
            for pr in range(8):
                qTp, qk = q_ring.next()
                ctx.dma('pool', qTp[:], qT_d[pr], W=[qk])
                kTp, kk = k_ring.next()
                ctx.dma('pool', kTp[:], kT_d[pr], W=[kk])
                va, vk = v_ring.next()
                for h2 in range(2):
                    ctx.dma('pool', va[:, :, h2, 0:64],
                            v_d[:, pr * 128 + h2 * 64:pr * 128 + h2 * 64 + 64].rearrange("(t p) d -> p t d", p=128), W=[vk])
                for h2 in range(2):
                    ctx.copy('dve', va[:, :, h2, 64], ones_sb[:, :], R=['onesb'], W=[vk])
                bt, bk = b_ring.next()
                ctx.dma('sp', bt[:], bt_d[2 * pr:2 * pr + 2].rearrange("h p q -> p h q"), W=[bk])
                for h2 in range(2):
                    ctx.memset('pool', bt[64:128, h2, 0:64], NEGM, W=[bk])
                    ctx.memset('pool', bt[0:64, h2, 576:640], NEGM, W=[bk])
                for qi, (q0, nq, kt0, samp) in enumerate(qtiles):
                    ot, otk = ot_ring.next()
                    for h2 in range(2):
                        hp = h2 * 64
                        S, Sk = s_ring.next()
                        for j in range(5):
                            nk = 64 if (samp and j == 4) else 128
                            kc = (kt0 + j) * 128
                            ctx.mm(S[0:nk, (4 - j) * 128:(4 - j) * 128 + nq], kTp[hp:hp + 64, kc:kc + nk],
                                   qTp[hp:hp + 64, q0:q0 + nq], True, True, R=[kk, qk], W=[Sk], sig=(j == 4))
                        tmp, tk = tmp_ring.next()
                        ctx.tt('dve', tmp[:, :], S[:, 0:640], bt[:, h2, :], ALU.add, R=[Sk, bk], W=[tk])
                        e, ek = e_ring.next()
                        ctx.act(e[:, :], tmp[:, :], AF.Exp, R=[tk], W=[ek])
                        num, nmk = n_ring.next()
                        for j in range(5):
                            nk = 64 if (samp and j == 4) else 128
                            ctx.mm(num[0:nq, 0:65], e[0:nk, (4 - j) * 128:(4 - j) * 128 + nq], va[0:nk, kt0 + j, h2, :],
                                   j == 0, j == 4, R=[ek, vk], W=[nmk])
                        rd, rdk = rd_ring.next()
                        ctx.op('dve', lambda: ctx.nc.vector.reciprocal(out=rd[0:nq, :], in_=num[0:nq, 64:65]),
                               R=[nmk], W=[rdk])
                        ctx.act(ot[0:nq, hp:hp + 64], num[0:nq, 0:64], AF.Copy, R=[nmk, rdk], W=[otk], scale=rd[0:nq, 0:1])
                    tp, tpk = tp_ring.next()
                    ctx.tr(tp[:, 0:nq], ot[0:nq, :], cs.ident_b[0:nq, 0:nq], R=[otk, 'cstb'], W=[tpk])
                    ctx.copy('dve', oT[:, pr, q0:q0 + nq], tp[:, 0:nq], R=[tpk], W=['oT%d' % qi])
        ctx.barrier()
        with ExitStack() as es:
            wo = ctx.tile('wo', [128, 8, D], BF16, es)
            load_w(ctx, wo[:], 'wo', wout_d, 0, D)
            po_ring = Ring(ctx, 'po', [128, 512], F32, 3, es, psum=True)
            for i, (t0, n) in enumerate(tiles_of(NT)):
                for dc in range(KC):
                    po, pok = po_ring.next()
                    for k in range(8):
                        ctx.mm(po[:, 0:n], wo[:, k, dc * 128:(dc + 1) * 128], oT[:, k, t0:t0 + n], k == 0, k == 7,
                               R=['wo'], W=[pok])
                    ctx.tt('dve', xT[:, dc, t0:t0 + n], xT[:, dc, t0:t0 + n], po[:, 0:n], ALU.add,
                           R=['x%d' % i, pok], W=['x%d' % i])
        ctx.barrier()


def z_stage(ctx, cs, xT, NT, g_d, winb_d, qkp_o, vz_o, oz_o, gz_o):
    NZ = 4 * D + 8
    with ExitStack() as es:
        g = ctx.tile('gmixb', [128, KC], F32, es)
        ctx.dma('sp', g[:], g_d, W=['gmixb'])
        w = ctx.tile('winb', [128, KC, NZ], BF16, es)
        for c in range(8):
            load_w(ctx, w[:, :, c * 512:(c + 1) * 512], 'winb%d' % c, winb_d, c * 512, (c + 1) * 512)
        load_w(ctx, w[:, :, 4096:NZ], 'winb8', winb_d, 4096, NZ)
        h_ring = Ring(ctx, 'hz', [128, KC, 512], BF16, 2, es)
        sq_ring = Ring(ctx, 'sq', [128, KC, 512], BF16, 2, es)
        ss_ring = Ring(ctx, 'ssp', [128, 512], F32, 2, es, psum=True)
        rs_ring = Ring(ctx, 'rs', [128, 512], F32, 2, es)
        pz_ring = Ring(ctx, 'pz', [128, 512], F32, 4, es, psum=True)
        st_ring = Ring(ctx, 'stz', [128, 512], F32, 4, es)
        for i, (t0, n) in enumerate(tiles_of(NT)):
            ht, hk = h_ring.next()
            rmsnorm_tile(ctx, cs, xT[:, :, t0:t0 + n], 'x%d' % i, g, 'gmixb', n, ht[:, :, 0:n], hk, sq_ring, ss_ring, rs_ring)
            for j in range(16):
                pz, pzk = pz_ring.next()
                for k in range(KC):
                    ctx.mm(pz[:, 0:n], w[:, k, j * 128:(j + 1) * 128], ht[:, k, 0:n], k == 0, k == KC - 1,
                           R=['winb%d' % (j // 4), hk], W=[pzk])
                st, stk = st_ring.next()
                ctx.copy('act' if j % 2 else 'dve', st[:, 0:n], pz[:, 0:n], R=[pzk], W=[stk])
                ctx.dma('sp', qkp_o[j, :, t0:t0 + n], st[:, 0:n], R=[stk])
            for s0 in range(0, n, 128):
                for gv in range(4):
                    pz, pzk = pz_ring.next()
                    for k in range(KC):
                        ctx.mm(pz[:, :], ht[:, k, s0:s0 + 128], w[:, k, 2048 + gv * 512:2048 + (gv + 1) * 512],
                               k == 0, k == KC - 1, R=['winb%d' % (4 + gv), hk], W=[pzk])
                    st, stk = st_ring.next()
                    ctx.copy('act' if gv % 2 else 'dve', st[:, :], pz[:, :], R=[pzk], W=[stk])
                    dst = (vz_o if gv < 2 else oz_o)[t0 + s0:t0 + s0 + 128, (gv % 2) * 512:(gv % 2 + 1) * 512]
                    ctx.dma('sp', dst, st[:, :], R=[stk])
                pz, pzk = pz_ring.next()
                for k in range(KC):
                    ctx.mm(pz[:, 0:8], ht[:, k, s0:s0 + 128], w[:, k, 4096:NZ], k == 0, k == KC - 1,
                           R=['winb8', hk], W=[pzk])
                st, stk = st_ring.next()
                ctx.copy('dve', st[:, 0:8], pz[:, 0:8], R=[pzk], W=[stk])
                ctx.dma('sp', gz_o[t0 + s0:t0 + s0 + 128, :], st[:, 0:8], R=[stk])
    ctx.barrier()


def build_attn(P, with_z):
    NT = P + 128
    NK = 512 + P + 1280
    nc = new_nc()
    x_d = din(nc, "xT", [KC, 128, NT])
    cst_d = din(nc, "cst", [128, 896])
    qT_d = din(nc, "qT", [8, 128, NT])
    kT_d = din(nc, "kT", [8, 128, NK])
    v_d = din(nc, "v", [NK, D])
    ones_d = din(nc, "ones", [128, NK // 128])
    bt_d = din(nc, "bt", [HA, 128, 640])
    wout_d = din(nc, "wout", [D, D])
    nf_d = din(nc, "nf", [128, KC])
    wup_d = din(nc, "wup", [D, DFF])
    wdn_d = din(nc, "wdn", [DFF, D])
    xo_d = dout(nc, "xo", [KC, 128, NT])
    if with_z:
        gb_d = din(nc, "gb", [128, KC])
        winb_d = din(nc, "winb", [D, 4 * D + 8])
        qkp_o = dout(nc, "qkp", [16, 128, NT])
        vz_o = dout(nc, "vz", [NT, D])
        oz_o = dout(nc, "oz", [NT, D])
        gz_o = dout(nc, "gz", [NT, 8])
    with ExitStack() as es:
        ctx = Ctx(nc, es)
        cs = Consts(ctx, cst_d)
        xT = ctx.tile('xT', [128, KC, NT], F32)
        load_x(ctx, xT, x_d, NT)
        attn_stage(ctx, cs, xT, P, qT_d, kT_d, v_d, ones_d, bt_d, wout_d)
        mlp_stage(ctx, cs, xT, NT, nf_d, wup_d, wdn_d)
        store_x(ctx, xT, xo_d, NT)
        if with_z:
            z_stage(ctx, cs, xT, NT, gb_d, winb_d, qkp_o, vz_o, oz_o, gz_o)
        ctx.finish()
    return nc


def fm(a):
    T, Fd = a.shape
    return np.ascontiguousarray(a.T.reshape(Fd // 128, 128, T))


def unfm(a):
    C, _, T = a.shape
    return np.ascontiguousarray(a.reshape(C * 128, T).T)


def colvec(g, k):
    return np.ascontiguousarray(g.reshape(k, 128).T.astype(np.float32))


_PROG = {}


def prog(key, builder):
    if key not in _PROG:
        _PROG[key] = builder()
    return _PROG[key]


def host_qkv(xT_list, NT, l, j, inp, cst):
    nc = build_qkv(NT)
    ims = []
    for c in range(NCORE):
        ims.append({"xT": xT_list[c], "cst": cst, "g": colvec(inp['norm_mix'][l], KC), "win": inp['w_in_a'][j],
                    "gq": np.tile(inp['q_norm'][j], 2).reshape(128, 1).astype(np.float32),
                    "gk": np.tile(inp['k_norm'][j], 2).reshape(128, 1).astype(np.float32)})
    return run_prog(nc, ims)


def assemble_attn(qkv, P, j, inp):
    kp = np.concatenate([unfm(r["kT"])[:P] for r in qkv], 0)
    vp = np.concatenate([r["v"][:P] for r in qkv], 0)
    SEQ = kp.shape[0]
    NK = 512 + P + 1280
    outs = []
    for c in range(NCORE):
        k_all = np.zeros((NK, D), np.float32)
        v_all = np.zeros((NK, D), np.float32)
        ones = np.zeros((NK,), np.float32)
        lo = c * P - 512
        s = max(lo, 0)
        k_all[s - lo:512] = kp[s:c * P]
        v_all[s - lo:512] = vp[s:c * P]
        ones[s - lo:512] = 1.0
        k_all[512:512 + P] = kp[c * P:(c + 1) * P]
        v_all[512:512 + P] = vp[c * P:(c + 1) * P]
        ones[512:512 + P] = 1.0
        ks = unfm(qkv[c]["kT"])
        for sidx in range(2):
            b = 2 * c + sidx
            o = 512 + P + 640 * sidx
            k_all[o:o + 512] = inp['cache_k'][j, b].reshape(512, D)
            v_all[o:o + 512] = inp['cache_v'][j, b].reshape(512, D)
            k_all[o + 512:o + 576] = ks[P + 64 * sidx:P + 64 * sidx + 64]
            v_all[o + 512:o + 576] = qkv[c]["v"][P + 64 * sidx:P + 64 * sidx + 64]
            ones[o:o + 576] = 1.0
        outs.append((qkv[c]["qT"], fm(k_all), v_all, np.ascontiguousarray(ones.reshape(NK // 128, 128).T)))
    return outs, kp, vp


def host_bt(rel_bias_l):
    kk = np.arange(128)[:, None]
    qq = np.arange(640)[None, :]
    idx = np.clip(qq - kk, -128, 128) + 128
    return np.ascontiguousarray(rel_bias_l[:, idx]).astype(np.float32)


def host_attn(xT_list, P, l, j, att, inp, cst, with_z, jb=None):
    nc = build_attn(P, with_z)
    bt = host_bt(inp['rel_bias'][j])
    ims = []
    for c in range(NCORE):
        qT, kT, v, ones = att[c]
        im = {"xT": xT_list[c], "cst": cst, "qT": qT, "kT": kT, "v": v, "ones": ones, "bt": bt,
              "wout": inp['w_out_a'][j], "nf": colvec(inp['norm_ffn'][l], KC), "wup": inp['w_up'][l],
              "wdn": inp['w_down'][l]}
        if with_z:
            im["gb"] = colvec(inp['norm_mix'][l + 1], KC)
            im["winb"] = inp['w_in_b'][jb]
        ims.append(im)
    return run_prog(nc, ims)


class MState:
    def __init__(self, ctx, es):
        self.CTf = ctx.tile('CTf', [128, 8, 257], F32, es)
        self.CTb = ctx.tile('CTb', [128, 8, 257], BF16, es)
        self.car = Ring(ctx, 'car', [128, 8], F32, 3, es)
        self.cur, self.curk = self.car.next()


def conv_silu(ctx, cs, src_d, c0, n, nf, f0, cw, cb, dst, dkey_fn, d0, scale_from, es):
    BL = 256
    if not hasattr(es, '_cv'):
        es._cv = (Ring(ctx, 'cvx', [128, nf, BL + 3], F32, 2, es), Ring(ctx, 'cva', [128, BL], F32, 3, es))
    x_ring, a_ring = es._cv
    for b0 in range(0, n, BL):
        bn = min(BL, n - b0)
        xt, xk = x_ring.next()
        ctx.dma('sp', xt[:, :, 0:bn + 3], src_d[:, :, c0 + b0:c0 + b0 + bn + 3].rearrange("f p t -> p f t"), W=[xk])
        for f in range(nf):
            ac, ak = a_ring.next()
            ctx.ts('dve', ac[:, 0:bn], xt[:, f, 0:bn], cw[:, f0 + f, 0:1], cb[:, f0 + f:f0 + f + 1], ALU.mult, ALU.add,
                   R=[xk, 'cwb'], W=[ak])
            for j in range(1, 4):
                ctx.stt('dve', ac[:, 0:bn], xt[:, f, j:j + bn], cw[:, f0 + f, j:j + 1], ac[:, 0:bn], ALU.mult, ALU.add,
                        R=[xk, 'cwb', ak], W=[ak])
            dk = dkey_fn(f)
            ctx.act(dst[:, f, d0 + b0:d0 + b0 + bn], ac[:, 0:bn], AF.Silu, R=[ak], W=[dk])
            if f0 + f >= scale_from:
                ctx.ts('pool', dst[:, f, d0 + b0:d0 + b0 + bn], dst[:, f, d0 + b0:d0 + b0 + bn], 1.0 / 16.0, None,
                       ALU.mult, None, R=[dk], W=[dk])


def mlstm_chunk(ctx, cs, M, T, full, qf, kf, col, qkkeys, v_src, o_src, g_src, bg, hn, hsT, hcol):
    c = ctx
    g, gk = T['g'].next()
    c.dma('sp', g[0:64, :], g_src, W=[gk])
    va, vk = T['va'].next()
    c.dma('pool', va[0:64, :, 0:256], v_src.rearrange("t (h e) -> t h e", h=4), W=[vk])
    if full:
        ot, ok = T['o'].next()
        c.dma('sp', ot[0:64, :], o_src, W=[ok])
    sm, smk = T['sm'].next()
    gb = sm[0:64, 0:8]
    c.tt('dve', gb, g[0:64, :], bg[0:64, :], ALU.add, R=[gk, 'bg'], W=[smk])
    e1 = sm[0:64, 8:12]
    c.act(e1, sm[0:64, 4:8], AF.Exp, R=[smk], W=[smk], scale=-1.0)
    sp = sm[0:64, 12:16]
    c.act(sp, e1, AF.Ln, R=[smk, 'cstcol'], W=[smk], bias=cs.one_col[0:64, :], scale=1.0)
    pg, pgk = T['pg'].next()
    c.mm(pg[0:64, 0:4], cs.U_f, sp, True, True, R=[smk, 'cstf'], W=[pgk])
    aprev = M.cur[:, 0:4]
    Bprev = M.cur[:, 4:8]
    aB = sm[0:64, 16:24]
    Bt = sm[0:64, 20:24]
    c.tt('dve', Bt, Bprev[0:64, :], pg[0:64, 0:4], ALU.subtract, R=[M.curk, pgk], W=[smk])
    Wc = sm[0:64, 24:28]
    c.tt('dve', Wc, sm[0:64, 0:4], Bt, ALU.subtract, R=[smk], W=[smk])
    dg, dgk = T['dg'].next()
    for h in range(4):
        c.ts('dve', dg[0:64, h, :], cs.ident_f[0:64, 0:64], Wc[:, h:h + 1], None, ALU.mult, None, R=['cstf', smk], W=[dgk])
    pr, prk = T['prow'].next()
    c.mm(pr[0:64, 0:256], cs.ones_f[0:64, 0:64], dg[0:64, :, :], True, True, R=[dgk, 'cstf'], W=[prk])
    ar, ark = T['arg'].next()
    for h in range(4):
        c.tt('dve', ar[0:64, h, :], pr[0:64, h * 64:(h + 1) * 64], cs.mneg_f, ALU.add, R=[prk, 'cstf'], W=[ark])
    cm = sm[0:64, 28:32]
    c.op('dve', lambda: c.nc.vector.tensor_reduce(out=cm, in_=ar[0:64, :, :], axis=AX.X, op=ALU.max), R=[ark], W=[smk])
    a = sm[0:64, 16:20]
    c.tt('dve', a, cm, aprev[0:64, :], ALU.max, R=[smk, M.curk], W=[smk])
    pg2, pg2k = T['pg'].next()
    c.mm(pg2[:, 0:8], cs.sel63_f, aB, True, True, R=[smk, 'cstf'], W=[pg2k])
    nxt, nxtk = M.car.next()
    c.copy('dve', nxt[:, :], pg2[:, 0:8], R=[pg2k], W=[nxtk])
    aL = nxt[:, 0:4]
    ex, exk = T['ex'].next()
    c.tt('dve', ex[0:64, 16:20], aprev[0:64, :], a, ALU.subtract, R=[M.curk, smk], W=[exk])
    c.tt('dve', ex[0:64, 20:24], Bt, a, ALU.add, R=[smk], W=[exk])
    c.ts('dve', ex[0:64, 20:24], ex[0:64, 20:24], -1.0, None, ALU.mult, None, R=[exk], W=[exk])
    c.tt('dve', ex[0:64, 24:28], Wc, aL[0:64, :], ALU.subtract, R=[smk, nxtk], W=[exk])
    c.act(ex[0:64, 0:12], ex[0:64, 16:28], AF.Exp, R=[exk], W=[exk])
    c.tt('dve', ex[:, 28:32], aprev, aL, ALU.subtract, R=[M.curk, nxtk], W=[exk])
    c.act(ex[:, 12:16], ex[:, 28:32], AF.Exp, R=[exk], W=[exk])
    inter = ex[0:64, 0:4]
    en = ex[0:64, 4:8]
    wS = ex[0:64, 8:12]
    decay = ex[:, 12:16]
    ptr, ptrk = T['ptr'].next()
    for i in range(8):
        c.tr(ptr[0:64, i * 128:(i + 1) * 128], kf[:, i, col:col + 64], cs.ident_b, R=qkkeys + ['cstb'], W=[ptrk], sig=(i == 7))
    kt, ktk = T['kt'].next()
    c.copy('act', kt[0:64, :, :], ptr[0:64, 0:1024], R=[ptrk], W=[ktk])
    if full:
        nega = sm[0:64, 32:36]
        c.ts('dve', nega, a, -1.0, None, ALU.mult, None, R=[smk], W=[smk])
        dg2, dg2k = T['dg'].next()
        for h in range(4):
            c.ts('dve', dg2[0:64, h, :], cs.ident_f[0:64, 0:64], nega[:, h:h + 1], None, ALU.mult, None,
                 R=['cstf', smk], W=[dg2k])
        pr2, pr2k = T['prow'].next()
        c.mm(pr2[0:64, 0:256], cs.ones_f[0:64, 0:64], dg2[0:64, :, :], True, True, R=[dg2k, 'cstf'], W=[pr2k])
        ar2, ar2k = T['arg'].next()
        for h in range(4):
            c.tt('dve', ar2[0:64, h, :], pr2[0:64, h * 64:(h + 1) * 64], cs.mnegT_f, ALU.add, R=[pr2k, 'cstf'], W=[ar2k])
        Dt, Dtk = T['Dt'].next()
        for h in range(4):
            c.act(Dt[0:64, h, :], ar2[0:64, h, :], AF.Exp, R=[ar2k, smk], W=[Dtk], bias=Wc[:, h:h + 1], scale=1.0)
        S, Sk = T['S'].next()
        for h in range(4):
            for dc in range(2):
                c.mm(S[0:64, h * 64:(h + 1) * 64], kf[:, 2 * h + dc, col:col + 64], qf[:, 2 * h + dc, col:col + 64],
                     dc == 0, dc == 1, R=qkkeys, W=[Sk], sig=(h == 3 and dc == 1))
        Sp, Spk = T['Sp'].next()
        c.tt('dve', Sp[0:64, :, :], S[0:64, 0:256], Dt[0:64, :, :], ALU.mult, R=[Sk, Dtk], W=[Spk])
        na, nak = T['na'].next()
        nb, nbk = T['nb'].next()
        for h in range(4):
            pa, pak = T['pA'].next()
            c.mm(pa[0:64, 0:257], Sp[0:64, h, :], va[0:64, h, :], True, True, R=[Spk, vk], W=[pak])
            c.copy('act', na[0:64, h, :], pa[0:64, 0:257], R=[pak], W=[nak])
            pb, pbk = T['pB'].next()
            for dc in range(2):
                c.mm(pb[0:64, 0:257], qf[:, 2 * h + dc, col:col + 64], M.CTb[:, 2 * h + dc, :], dc == 0, dc == 1,
                     R=qkkeys + ['CTb'], W=[pbk])
            c.copy('dve', nb[0:64, h, :], pb[0:64, 0:257], R=[pbk], W=[nbk])
        for h in range(4):
            c.stt('dve', na[0:64, h, :], nb[0:64, h, :], inter[:, h:h + 1], na[0:64, h, :], ALU.mult, ALU.add,
                  R=[nbk, exk, nak], W=[nak])
        dd = sm[0:64, 36:40]
        c.ts('dve', dd, na[0:64, :, 256], -1.0, None, ALU.mult, None, R=[nak], W=[smk])
        c.tt('dve', dd, dd, na[0:64, :, 256], ALU.max, R=[nak, smk], W=[smk])
        c.tt('dve', dd, dd, en, ALU.max, R=[smk, exk], W=[smk])
        rr = sm[0:64, 40:44]
        c.op('dve', lambda: c.nc.vector.reciprocal(out=rr, in_=dd), R=[smk], W=[smk])
        ssq = sm[0:64, 44:48]
        jk, jkk = T['junk'].next()
        for h in range(4):
            c.act(jk[0:64, 0:256], na[0:64, h, 0:256], AF.Square, R=[nak, smk], W=[jkk, smk], scale=rr[:, h:h + 1],
                  accum_out=ssq[:, h:h + 1])
        rst = sm[0:64, 48:52]
        c.act(rst, ssq, AF.Ln, R=[smk, 'cstcol'], W=[smk], bias=cs.eps_col[0:64, :], scale=1.0 / 256.0)
        c.act(rst, rst, AF.Exp, R=[smk], W=[smk], scale=-0.5)
        comb = sm[0:64, 52:56]
        c.tt('dve', comb, rr, rst, ALU.mult, R=[smk], W=[smk])
        sg, sgk = T['sig'].next()
        c.act(sg[0:64, :], ot[0:64, :], AF.Sigmoid, R=[ok], W=[sgk])
        t1, t1k = T['t1'].next()
        for h in range(4):
            c.stt('dve', t1[0:64, h * 256:(h + 1) * 256], na[0:64, h, 0:256], comb[:, h:h + 1],
                  hn[0:64, h * 256:(h + 1) * 256], ALU.mult, ALU.mult, R=[nak, smk, 'hn'], W=[t1k])
        hs, hsk = T['hs'].next()
        c.tt('pool', hs[0:64, :], t1[0:64, :], sg[0:64, :], ALU.mult, R=[t1k, sgk], W=[hsk])
        ptr2, ptr2k = T['ptr'].next()
        for i in range(8):
            c.tr(ptr2[:, i * 64:(i + 1) * 64], hs[0:64, i * 128:(i + 1) * 128], cs.ident_b[0:64, 0:64],
                 R=[hsk, 'cstb'], W=[ptr2k], sig=(i == 7))
        hb, hbk = T['hb'].next()
        c.copy('act', hb[:, :, :], ptr2[:, 0:512].rearrange("p (k t) -> p k t", k=8), R=[ptr2k], W=[hbk])
        c.dma('sp', hsT[:, :, hcol:hcol + 64].rearrange("k p t -> p k t"), hb[:, :, :], R=[hbk])
    wv, wvk = T['wv'].next()
    for h in range(4):
        c.ts('dve', wv[0:64, h, :], va[0:64, h, :], wS[:, h:h + 1], None, ALU.mult, None, R=[vk, exk], W=[wvk])
    for h in range(4):
        for dc in range(2):
            pu, puk = T['pU'].next()
            c.mm(pu[:, 0:257], kt[0:64, 2 * h + dc, :], wv[0:64, h, :], True, True, R=[ktk, wvk], W=[puk])
            c.stt('dve', M.CTf[:, 2 * h + dc, :], M.CTf[:, 2 * h + dc, :], decay[:, h:h + 1], pu[:, 0:257],
                  ALU.mult, ALU.add, R=['CTf', exk, puk], W=['CTf'])
    c.copy('pool', M.CTb[:, :, :], M.CTf[:, :, :], R=['CTf'], W=['CTb'])
    M.cur, M.curk = nxt, nxtk


def mlstm_work(ctx, es, full):
    T = {}
    T['g'] = Ring(ctx, 'mg', [128, 8], F32, 2, es)
    T['va'] = Ring(ctx, 'mva', [128, 4, 257], BF16, 2, es)
    for b in T['va'].bufs:
        ctx.memset('pool', b[:, :, 256:257], 1.0, W=[])
    T['sm'] = Ring(ctx, 'msm', [128, 64], F32, 2, es)
    T['dg'] = Ring(ctx, 'mdg', [128, 4, 64], F32, 2, es)
    T['arg'] = Ring(ctx, 'marg', [128, 4, 64], F32, 2, es)
    T['ex'] = Ring(ctx, 'mex', [128, 32], F32, 2, es)
    T['kt'] = Ring(ctx, 'mkt', [128, 8, 128], BF16, 2, es)
    T['wv'] = Ring(ctx, 'mwv', [128, 4, 257], BF16, 2, es)
    T['pg'] = Ring(ctx, 'mpg', [128, 512], F32, 1, es, psum=True)
    T['prow'] = Ring(ctx, 'mprow', [128, 512], F32, 1, es, psum=True)
    T['ptr'] = Ring(ctx, 'mptr', [128, 1024], BF16, 1, es, psum=True)
    T['pU'] = Ring(ctx, 'mpU', [128, 512], F32, 2, es, psum=True)
    if full:
        T['o'] = Ring(ctx, 'mo', [128, 1024], F32, 2, es)
        T['Dt'] = Ring(ctx, 'mDt', [128, 4, 64], F32, 1, es)
        T['Sp'] = Ring(ctx, 'mSp', [128, 4, 64], BF16, 1, es)
        T['na'] = Ring(ctx, 'mna', [128, 4, 257], F32, 1, es)
        T['nb'] = Ring(ctx, 'mnb', [128, 4, 257], F32, 1, es)
        T['junk'] = Ring(ctx, 'mjk', [128, 256], F32, 1, es)
        T['sig'] = Ring(ctx, 'msig', [128, 1024], F32, 1, es)
        T['t1'] = Ring(ctx, 'mt1', [128, 1024], F32, 1, es)
        T['hs'] = Ring(ctx, 'mhs', [128, 1024], BF16, 2, es)
        T['hb'] = Ring(ctx, 'mhb', [128, 8, 64], BF16, 2, es)
        T['S'] = Ring(ctx, 'mS', [128, 512], F32, 1, es, psum=True)
        T['pA'] = Ring(ctx, 'mpA', [128, 512], F32, 1, es, psum=True)
        T['pB'] = Ring(ctx, 'mpB', [128, 512], F32, 1, es, psum=True)
    return T


def store_state(ctx, M, CT_o, sc_o, es):
    ctx.dma('sp', CT_o.rearrange("k p e -> p k e"), M.CTf[:, :, :], R=['CTf'])
    so = ctx.tile('scout', [128, 12], F32, es)
    ctx.copy('dve', so[:, 0:8], M.cur[:, :], R=[M.curk], W=['scout'])
    ctx.tt('dve', so[:, 8:12], M.cur[:, 0:4], M.cur[:, 4:8], ALU.add, R=[M.curk], W=['scout'])
    ctx.dma('sp', sc_o, so[:, :], R=['scout'])


def build_pass1(P):
    nc = new_nc()
    cst_d = din(nc, "cst", [128, 896])
    kp_d = din(nc, "kp", [8, 128, P + 3])
    vz_d = din(nc, "vz", [P, D])
    gz_d = din(nc, "gz", [P, 8])
    cw_d = din(nc, "cw", [128, 16, 4])
    cb_d = din(nc, "cb", [128, 16])
    bg_d = din(nc, "bg", [128, 8])
    CT_o = dout(nc, "CT", [8, 128, 257])
    sc_o = dout(nc, "sc", [128, 12])
    with ExitStack() as es:
        ctx = Ctx(nc, es)
        cs = Consts(ctx, cst_d)
        cw = ctx.tile('cw', [128, 16, 4], F32)
        cb = ctx.tile('cb', [128, 16], F32)
        bg = ctx.tile('bg', [128, 8], F32)
        ctx.dma('sp', cw[:], cw_d, W=['cwb'])
        ctx.dma('sp', cb[:], cb_d, W=['cwb'])
        ctx.dma('sp', bg[:], bg_d, W=['bg'])
        kf = ctx.tile('kf', [128, 8, P], BF16)
        with ExitStack() as es2:
            conv_silu(ctx, cs, kp_d, 0, P, 8, 8, cw, cb, kf, lambda f: 'kf', 0, 8, es2)
        ctx.barrier()
        M = MState(ctx, es)
        ctx.memset('dve', M.CTf[:, :, :], 0.0, W=['CTf'])
        ctx.memset('pool', M.CTb[:, :, :], 0.0, W=['CTb'])
        ctx.memset('dve', M.cur[:, 0:4], A_INIT, W=[M.curk])
        ctx.memset('dve', M.cur[:, 4:8], 0.0, W=[M.curk])
        T = mlstm_work(ctx, es, False)
        for ch in range(P // 64):
            t0 = ch * 64
            mlstm_chunk(ctx, cs, M, T, False, None, kf, t0, ['kf'], vz_d[t0:t0 + 64, :], None, gz_d[t0:t0 + 64, :],
                        bg, None, None, 0)
        store_state(ctx, M, CT_o, sc_o, es)
        ctx.finish()
    return nc


def combine_stage(ctx, cs, M, CSall_d, scall_d, cm_d, es):
    sc = ctx.tile('scall', [128, 64], F32, es)
    cm = ctx.tile('cmk', [128, 16], F32, es)
    ctx.dma('sp', sc[:], scall_d, W=['cmb'])
    ctx.dma('sp', cm[:], cm_d, W=['cmb'])
    w = ctx.tile('cmbw', [128, 96], F32, es)
    K = ['cmb']
    ctx.memset('dve', w[:, 0:16], 0.0, W=K)
    for r in range(8):
        e = w[:, 16 + 4 * r:20 + 4 * r]
        ctx.tt('dve', e, sc[:, r * 8:r * 8 + 4], w[:, 0:4], ALU.subtract, R=K, W=K)
        ctx.ts('dve', e, e, cm[:, 8 + r:9 + r], None, ALU.add, None, R=K, W=K)
        ctx.tt('dve', w[:, 0:4], w[:, 0:4], sc[:, r * 8 + 4:r * 8 + 8], ALU.add, R=K, W=K)
        ctx.tt('dve', w[:, 4:8], w[:, 4:8], e, ALU.max, R=K, W=K)
        ctx.stt('dve', w[:, 8:12], sc[:, r * 8 + 4:r * 8 + 8], cm[:, r:r + 1], w[:, 8:12], ALU.mult, ALU.add, R=K, W=K)
    for r in range(8):
        ctx.tt('dve', w[:, 48 + 4 * r:52 + 4 * r], w[:, 16 + 4 * r:20 + 4 * r], w[:, 4:8], ALU.subtract, R=K, W=K)
    ctx.act(w[:, 48:80], w[:, 48:80], AF.Exp, R=K, W=K)
    ctx.tt('dve', M.cur[:, 0:4], w[:, 4:8], w[:, 8:12], ALU.add, R=K, W=[M.curk])
    ctx.memset('dve', M.cur[:, 4:8], 0.0, W=[M.curk])
    ctx.memset('dve', M.CTf[:, :, :], 0.0, W=['CTf'])
    stg_ring = Ring(ctx, 'csstg', [128, 8, 257], F32, 2, es)
    for r in range(8):
        stg, sk = stg_ring.next()
        ctx.dma('sp', stg[:, :, :], CSall_d[r].rearrange("k p e -> p k e"), W=[sk])
        for hd in range(8):
            ctx.stt('dve', M.CTf[:, hd, :], stg[:, hd, :], w[:, 48 + 4 * r + hd // 2:49 + 4 * r + hd // 2], M.CTf[:, hd, :],
                    ALU.mult, ALU.add, R=[sk, 'CTf'] + K, W=['CTf'])
    ctx.copy('pool', M.CTb[:, :, :], M.CTf[:, :, :], R=['CTf'], W=['CTb'])


def build_pass2(P, with_qkv):
    NT = P + 128
    NQ = P + 3 + 134
    nc = new_nc()
    cst_d = din(nc, "cst", [128, 896])
    x_d = din(nc, "xT", [KC, 128, NT])
    qkp_d = din(nc, "qkp", [16, 128, NQ])
    vz_d = din(nc, "vz", [NT, D])
    oz_d = din(nc, "oz", [NT, D])
    gz_d = din(nc, "gz", [NT, 8])
    cw_d = din(nc, "cw", [128, 16, 4])
    cb_d = din(nc, "cb", [128, 16])
    bg_d = din(nc, "bg", [128, 8])
    hn_d = din(nc, "hn", [128, D])
    CSall_d = din(nc, "CSall", [8, 8, 128, 257])
    scall_d = din(nc, "scall", [128, 64])
    cm_d = din(nc, "cm", [128, 16])
    CTs_d = din(nc, "CTs", [2, 8, 128, 257])
    ms_d = din(nc, "ms", [128, 8])
    woutb_d = din(nc, "woutb", [D, D])
    nf_d = din(nc, "nf", [128, KC])
    wup_d = din(nc, "wup", [D, DFF])
    wdn_d = din(nc, "wdn", [DFF, D])
    xo_d = dout(nc, "xo", [KC, 128, NT])
    CTp_o = dout(nc, "CTp", [8, 128, 257])
    scp_o = dout(nc, "scp", [128, 12])
    CTs_o = dout(nc, "CTso", [2, 8, 128, 257])
    scs_o = dout(nc, "scs", [2, 128, 12])
    if with_qkv:
        g2_d = din(nc, "g", [128, KC])
        win_d = din(nc, "win", [D, 3 * D])
        gq_d = din(nc, "gq", [128, 1])
        gk_d = din(nc, "gk", [128, 1])
        qT_o = dout(nc, "qT", [8, 128, NT])
        kT_o = dout(nc, "kT", [8, 128, NT])
        v_o = dout(nc, "v", [NT, D])
    with ExitStack() as es:
        ctx = Ctx(nc, es)
        cs = Consts(ctx, cst_d)
        hsT = nc.dram_tensor("hsT_scr", [8, 128, NT], BF16).ap()
        with ExitStack() as es1:
            cw = ctx.tile('cw', [128, 16, 4], F32, es1)
            cb = ctx.tile('cb', [128, 16], F32, es1)
            bg = ctx.tile('bg', [128, 8], F32, es1)
            hn = ctx.tile('hn', [128, D], F32, es1)
            ms = ctx.tile('ms', [128, 8], F32, es1)
            ctx.dma('sp', cw[:], cw_d, W=['cwb'])
            ctx.dma('sp', cb[:], cb_d, W=['cwb'])
            ctx.dma('sp', bg[:], bg_d, W=['bg'])
            ctx.dma('sp', hn[:], hn_d, W=['hn'])
            ctx.dma('sp', ms[:], ms_d, W=['ms'])
            qkf = ctx.tile('qkf', [128, 16, NT], BF16, es1)
            with ExitStack() as es2:
                conv_silu(ctx, cs, qkp_d, 0, P, 16, 0, cw, cb, qkf, lambda f: 'qkf', 0, 8, es2)
                for s in range(2):
                    conv_silu(ctx, cs, qkp_d, P + 3 + 67 * s, 64, 16, 0, cw, cb, qkf, lambda f: 'qkf', P + 64 * s, 8, es2)
            ctx.barrier()
            M = MState(ctx, es1)
            with ExitStack() as es2:
                combine_stage(ctx, cs, M, CSall_d, scall_d, cm_d, es2)
            ctx.barrier()
            T = mlstm_work(ctx, es1, True)
            qf = qkf[:, 0:8, :]
            kf = qkf[:, 8:16, :]
            for ch in range(P // 64):
                t0 = ch * 64
                mlstm_chunk(ctx, cs, M, T, True, qf, kf, t0, ['qkf'], vz_d[t0:t0 + 64, :], oz_d[t0:t0 + 64, :],
                            gz_d[t0:t0 + 64, :], bg, hn, hsT, t0)
            store_state(ctx, M, CTp_o, scp_o, es1)
            for s in range(2):
                t0 = P + 64 * s
                ctx.dma('sp', M.CTf[:, :, :], CTs_d[s].rearrange("k p e -> p k e"), W=['CTf'])
                ctx.copy('pool', M.CTb[:, :, :], M.CTf[:, :, :], R=['CTf'], W=['CTb'])
                M.cur, M.curk = M.car.next()
                ctx.copy('dve', M.cur[:, 0:4], ms[:, 4 * s:4 * s + 4], R=['ms'], W=[M.curk])
                ctx.memset('dve', M.cur[:, 4:8], 0.0, W=[M.curk])
                mlstm_chunk(ctx, cs, M, T, True, qf, kf, t0, ['qkf'], vz_d[t0:t0 + 64, :], oz_d[t0:t0 + 64, :],
                            gz_d[t0:t0 + 64, :], bg, hn, hsT, t0)
                store_state(ctx, M, CTs_o[s], scs_o[s], es1)
        ctx.barrier()
        xT = ctx.tile('xT', [128, KC, NT], F32)
        load_x(ctx, xT, x_d, NT)
        with ExitStack() as es1:
            wo = ctx.tile('wob', [128, 8, D], BF16, es1)
            load_w(ctx, wo[:], 'wob', woutb_d, 0, D)
            po_ring = Ring(ctx, 'pob', [128, 512], F32, 3, es1, psum=True)
            hl_ring = Ring(ctx, 'hsl', [128, 8, 512], BF16, 2, es1)
            for i, (t0, n) in enumerate(tiles_of(NT)):
                hl, hlk = hl_ring.next()
                ctx.dma('sp', hl[:, :, 0:n], hsT[:, :, t0:t0 + n].rearrange("k p t -> p k t"), W=[hlk])
                for dc in range(KC):
                    po, pok = po_ring.next()
                    for k in range(8):
                        ctx.mm(po[:, 0:n], wo[:, k, dc * 128:(dc + 1) * 128], hl[:, k, 0:n], k == 0, k == 7,
                               R=['wob', hlk], W=[pok])
                    ctx.tt('dve', xT[:, dc, t0:t0 + n], xT[:, dc, t0:t0 + n], po[:, 0:n], ALU.add,
                           R=['x%d' % i, pok], W=['x%d' % i])
        ctx.barrier()
        mlp_stage(ctx, cs, xT, NT, nf_d, wup_d, wdn_d)
        store_x(ctx, xT, xo_d, NT)
        if with_qkv:
            qkv_stage(ctx, cs, None, NT, g2_d, win_d, gq_d, gk_d, qT_o, kT_o, v_o, x_resident=xT)
        ctx.finish()
    return nc


def kernel(**inputs):
    inp = {k: np.asarray(v) for k, v in inputs.items()}
    SEQ = inp['x_prompt'].shape[1]
    P = SEQ // NCORE
    NT = P + 128
    cst = host_consts()
    xT = [fm(np.concatenate([inp['x_prompt'][0, c * P:(c + 1) * P], inp['x_sample'][2 * c], inp['x_sample'][2 * c + 1]], 0)
             .astype(np.float32)) for c in range(NCORE)]
    keep = min(WIN, SEQ)
    kp_l, vp_l, ks_l, vs_l = [], [], [], []
    Cp_l, np_l, mp_l, bp_l, Cs_l, ns_l, ms_l, bs_l = [], [], [], [], [], [], [], []
    qkv = None
    for l in range(4):
        j = l // 2
        if l % 2 == 0:
            if qkv is None:
                qkv = host_qkv(xT, NT, l, j, inp, cst)
            att, kp, vp = assemble_attn(qkv, P, j, inp)
            res = host_attn(xT, P, l, j, att, inp, cst, True, j)
            kp_l.append(kp[-keep:].reshape(1, keep, HA, HDA))
            vp_l.append(vp[-keep:].reshape(1, keep, HA, HDA))
            ks = np.stack([unfm(qkv[b // 2]["kT"])[P + 64 * (b % 2):P + 64 * (b % 2) + 64] for b in range(16)], 0)
            vs = np.stack([qkv[b // 2]["v"][P + 64 * (b % 2):P + 64 * (b % 2) + 64] for b in range(16)], 0)
            ks_l.append(ks.reshape(16, 64, HA, HDA))
            vs_l.append(vs.reshape(16, 64, HA, HDA))
            xT = [r["xo"] for r in res]
            z = res
            qkv = None
        else:
            qk_loc = [unfm(z[c]["qkp"]) for c in range(NCORE)]
            qk_glob = np.concatenate([a[:P] for a in qk_loc], 0)
            cw = inp['conv_w'][j]
            cwl = np.ascontiguousarray(cw.T.reshape(16, 128, 4).transpose(1, 0, 2)).astype(np.float32)
            cbl = colvec(inp['conv_b'][j], 16)
            bgl = np.tile(np.concatenate([inp['b_gate_i'][j], inp['b_gate_f'][j]])[None], (128, 1)).astype(np.float32)
            halo = []
            for c in range(NCORE):
                hl = np.zeros((3, 2 * D), np.float32)
                if c > 0:
                    hl[:] = qk_glob[c * P - 3:c * P]
                halo.append(hl)
            nc1 = build_pass1(P)
            ims = []
            for c in range(NCORE):
                kpre = np.concatenate([halo[c][:, D:], qk_loc[c][:P, D:]], 0)
                ims.append({"cst": cst, "kp": fm(kpre), "vz": np.ascontiguousarray(z[c]["vz"][:P]),
                            "gz": np.ascontiguousarray(z[c]["gz"][:P]), "cw": cwl, "cb": cbl, "bg": bgl})
            r1 = run_prog(nc1, ims)
            CSall = np.stack([r["CT"] for r in r1], 0)
            scall = np.concatenate([r["sc"][:, 0:8] for r in r1], 1)
            with_qkv = (l == 1)
            nc2 = build_pass2(P, with_qkv)
            ims = []
            for c in range(NCORE):
                parts = [halo[c], qk_loc[c][:P]]
                for s in range(2):
                    parts += [inp['state_conv'][j, 2 * c + s], qk_loc[c][P + 64 * s:P + 64 * s + 64]]
                qkp_in = fm(np.concatenate(parts, 0).astype(np.float32))
                cm = np.zeros((16,), np.float32)
                cm[:8] = (np.arange(8) < c).astype(np.float32)
                cm[8:] = np.where(np.arange(8) < c, 0.0, NEGBIG)
                CTs = np.zeros((2, 8, 128, 257), np.float32)
                msv = np.zeros((8,), np.float32)
                for s in range(2):
                    b = 2 * c + s
                    Ct = inp['state_C'][j, b].transpose(0, 2, 1)
                    CTs[s, :, :, :256] = Ct.reshape(8, 128, 256)
                    CTs[s, :, :, 256] = inp['state_n'][j, b].reshape(8, 128)
                    msv[4 * s:4 * s + 4] = inp['state_m'][j, b]
                im = {"cst": cst, "xT": xT[c], "qkp": qkp_in, "vz": z[c]["vz"], "oz": z[c]["oz"], "gz": z[c]["gz"],
                      "cw": cwl, "cb": cbl, "bg": bgl,
                      "hn": np.tile(inp['head_norm'][j][None], (128, 1)).astype(np.float32),
                      "CSall": CSall, "scall": scall, "cm": np.tile(cm[None], (128, 1)),
                      "CTs": CTs, "ms": np.tile(msv[None], (128, 1)), "woutb": inp['w_out_b'][j],
                      "nf": colvec(inp['norm_ffn'][l], KC), "wup": inp['w_up'][l], "wdn": inp['w_down'][l]}
                if with_qkv:
                    im.update({"g": colvec(inp['norm_mix'][l + 1], KC), "win": inp['w_in_a'][j + 1],
                               "gq": np.tile(inp['q_norm'][j + 1], 2).reshape(128, 1).astype(np.float32),
                               "gk": np.tile(inp['k_norm'][j + 1], 2).reshape(128, 1).astype(np.float32)})
                ims.append(im)
            r2 = run_prog(nc2, ims)
            xT = [r["xo"] for r in r2]
            if with_qkv:
                qkv = r2
            last = r2[NCORE - 1]
            ct = last["CTp"].reshape(HB, 256, 257)
            Cp_l.append(np.ascontiguousarray(ct[:, :, :256].transpose(0, 2, 1))[None])
            np_l.append(np.ascontiguousarray(ct[:, :, 256])[None])
            mp_l.append(last["scp"][0, 8:12][None].copy())
            bp_l.append(qk_glob[-3:][None].copy())
            Cs, ns, mss, bss = [], [], [], []
            for b in range(16):
                r = r2[b // 2]
                s = b % 2
                ct = r["CTso"][s].reshape(HB, 256, 257)
                Cs.append(np.ascontiguousarray(ct[:, :, :256].transpose(0, 2, 1)))
                ns.append(np.ascontiguousarray(ct[:, :, 256]))
                mss.append(r["scs"][s][0, 8:12].copy())
                bss.append(qk_loc[b // 2][P + 64 * s + 61:P + 64 * s + 64].copy())
            Cs_l.append(np.stack(Cs)); ns_l.append(np.stack(ns)); ms_l.append(np.stack(mss)); bs_l.append(np.stack(bss))
    yp = np.concatenate([unfm(xT[c])[:P] for c in range(NCORE)], 0)[None]
    ys = np.stack([unfm(xT[b // 2])[P + 64 * (b % 2):P + 64 * (b % 2) + 64] for b in range(16)], 0)
    f = lambda a: np.ascontiguousarray(np.stack(a).astype(np.float32))
    return (yp.astype(np.float32), ys.astype(np.float32), f(kp_l), f(vp_l), f(ks_l), f(vs_l),
            f(Cp_l), f(np_l), f(mp_l), f(bp_l), f(Cs_l), f(ns_l), f(ms_l), f(bs_l))
```
